# Optimizing a Trainium2 kernel written in Bass

```python
import math
import jax, jax.numpy as jnp
from jax import lax
import numpy as np

D_MODEL = 1024
BATCH = 2
SEQ = 8192
DEPTH = 4

CHUNK = 64
MIX_WIDTH = D_MODEL
POOL_WIDTH = D_MODEL // 2
POOL_GROUPS = 4
POOL_GROUP_DIM = POOL_WIDTH // POOL_GROUPS
POOL_WINDOWS = (2, 4, 8, 16)
GDN_WIDTH = MIX_WIDTH - POOL_WIDTH
GDN_HEADS = 4
GDN_HEAD_DIM = GDN_WIDTH // GDN_HEADS
CONV_WIDTH = 4
D_FF = 4 * D_MODEL
EPS = 1e-6
IN_WIDTH = POOL_WIDTH + 4 * GDN_WIDTH + 2 * GDN_HEADS

kernel_name = "hybrid_pool_gdn_trunk"


def rms_norm(x, w):
    xf = x.astype(jnp.float32)
    y = xf * lax.rsqrt(jnp.mean(xf * xf, axis=-1, keepdims=True) + EPS)
    return (y * w.astype(jnp.float32)).astype(x.dtype)


def l2_norm(t):
    return t * lax.rsqrt(jnp.sum(t * t, axis=-1, keepdims=True) + EPS)


def pool_mixer(u, w_pool, pool_scale):
    B, S, _ = u.shape
    uf = u.astype(jnp.float32)
    cs = jnp.cumsum(uf, axis=1)
    count = jnp.arange(1, S + 1, dtype=jnp.float32)[None, :, None]
    outs = []
    for g, w in enumerate(POOL_WINDOWS):
        sl = slice(g * POOL_GROUP_DIM, (g + 1) * POOL_GROUP_DIM)
        c = cs[..., sl]
        prev = jnp.pad(c, ((0, 0), (w, 0), (0, 0)))[:, :S]
        mean = (c - prev) / jnp.minimum(count, float(w))
        outs.append(mean - uf[..., sl])
    pooled = jnp.stack(outs, axis=2).astype(u.dtype)
    mixed = jnp.einsum('bsgc,gcd->bsgd', pooled, w_pool)
    return mixed.reshape(B, S, POOL_WIDTH) * pool_scale


def causal_conv_silu(x, w):
    C = x.shape[-1]
    y = lax.conv_general_dilated(x, w[:, None, :].astype(x.dtype), window_strides=(1,),
                                 padding=[(CONV_WIDTH - 1, 0)],
                                 dimension_numbers=('NWC', 'WIO', 'NWC'),
                                 feature_group_count=C)
    return jax.nn.silu(y)


def gated_delta_net(q, k, v, z, a, b, a_log, dt_bias, gdn_norm):
    B, S, _ = q.shape
    H, Dh, C = GDN_HEADS, GDN_HEAD_DIM, CHUNK
    N = S // C
    f32 = jnp.float32

    def heads(t):
        return t.astype(f32).reshape(B, N, C, H, Dh).transpose(0, 3, 1, 2, 4)

    def per_head(t):
        return t.reshape(B, N, C, H).transpose(0, 3, 1, 2)

    qh = l2_norm(heads(q)) * (Dh ** -0.5)
    kh = l2_norm(heads(k))
    vh = heads(v)
    beta = per_head(jax.nn.sigmoid(b.astype(f32)))
    g = -jnp.exp(a_log.astype(f32)) * jax.nn.softplus(a.astype(f32) + dt_bias.astype(f32))
    gcum = jnp.cumsum(per_head(g), axis=-1)

    idx = jnp.arange(C)
    incl = idx[:, None] >= idx[None, :]
    strict = idx[:, None] > idx[None, :]
    diff = gcum[..., :, None] - gcum[..., None, :]
    decay_incl = jnp.exp(jnp.where(incl, diff, -jnp.inf))
    decay_strict = jnp.where(strict, decay_incl, 0.0)

    kb = kh * beta[..., None]
    m = jnp.einsum('bhncd,bhnjd->bhncj', kb, kh) * decay_strict
    lhs = jnp.eye(C, dtype=f32) + m
    rhs = jnp.concatenate([vh * beta[..., None], kb * jnp.exp(gcum)[..., None]], axis=-1)
    sol = lax.linalg.triangular_solve(lhs, rhs, left_side=True, lower=True)
    value, k_cumdecay = sol[..., :Dh], sol[..., Dh:]
    attn_inner = jnp.einsum('bhncd,bhnjd->bhncj', qh, kh) * decay_incl
    q_dec = qh * jnp.exp(gcum)[..., None]
    g_last = gcum[..., -1]
    k_dec = kh * jnp.exp(g_last[..., None] - gcum)[..., None]

    def step(state, inp):
        qd, kc, val, ai, kd, gl = inp
        v_new = val - jnp.einsum('bhcd,bhde->bhce', kc, state)
        o = jnp.einsum('bhcd,bhde->bhce', qd, state) + jnp.einsum('bhcj,bhje->bhce', ai, v_new)
        state = state * jnp.exp(gl)[..., None, None] + jnp.einsum('bhcd,bhce->bhde', kd, v_new)
        return state, o

    xs = (jnp.moveaxis(q_dec, 2, 0), jnp.moveaxis(k_cumdecay, 2, 0), jnp.moveaxis(value, 2, 0),
          jnp.moveaxis(attn_inner, 2, 0), jnp.moveaxis(k_dec, 2, 0), jnp.moveaxis(g_last, 2, 0))
    state0 = jnp.zeros((B, H, Dh, Dh), f32)
    _, o = lax.scan(step, state0, xs)
    o = o.transpose(1, 0, 3, 2, 4).reshape(B, S, H, Dh)
    o = o * lax.rsqrt(jnp.mean(o * o, axis=-1, keepdims=True) + EPS) * gdn_norm.astype(f32)
    o = o * jax.nn.silu(z.astype(f32).reshape(B, S, H, Dh))
    return o.reshape(B, S, GDN_WIDTH).astype(q.dtype)


def setup_inputs(seed: int = 0) -> dict:
    key = jax.random.key(seed)
    ks = jax.random.split(key, 16)
    nrm = jax.random.normal
    H, Dh = GDN_HEADS, GDN_HEAD_DIM
    x = nrm(ks[0], (BATCH, SEQ, D_MODEL), jnp.float32)
    norm_mix = 1.0 + 0.02 * nrm(ks[1], (DEPTH, D_MODEL), jnp.float32)
    w_in = nrm(ks[2], (DEPTH, D_MODEL, IN_WIDTH), jnp.float32) * D_MODEL ** -0.5
    conv_w = nrm(ks[3], (DEPTH, CONV_WIDTH, 3 * GDN_WIDTH), jnp.float32) * CONV_WIDTH ** -0.5
    w_pool = nrm(ks[4], (DEPTH, POOL_GROUPS, POOL_GROUP_DIM, POOL_GROUP_DIM), jnp.float32) * POOL_GROUP_DIM ** -0.5
    pool_scale = 1.0 + 0.1 * nrm(ks[5], (DEPTH, POOL_WIDTH), jnp.float32)
    a_log = jnp.log(jax.random.uniform(ks[6], (DEPTH, H), jnp.float32, 1.0, 16.0))
    dt = jnp.exp(jax.random.uniform(ks[7], (DEPTH, H), jnp.float32, math.log(1e-3), math.log(1e-1)))
    dt_bias = dt + jnp.log(-jnp.expm1(-dt))
    gdn_norm = 1.0 + 0.02 * nrm(ks[8], (DEPTH, Dh), jnp.float32)
    w_out = nrm(ks[9], (DEPTH, MIX_WIDTH, D_MODEL), jnp.float32) * MIX_WIDTH ** -0.5
    norm_mlp = 1.0 + 0.02 * nrm(ks[10], (DEPTH, D_MODEL), jnp.float32)
    w_up = nrm(ks[11], (DEPTH, D_MODEL, D_FF), jnp.float32) * D_MODEL ** -0.5
    w_down = nrm(ks[12], (DEPTH, D_FF, D_MODEL), jnp.float32) * D_FF ** -0.5
    norm_final = 1.0 + 0.02 * nrm(ks[13], (D_MODEL,), jnp.float32)
    return {"x": x, "norm_mix": norm_mix, "w_in": w_in, "conv_w": conv_w, "w_pool": w_pool,
            "pool_scale": pool_scale, "a_log": a_log, "dt_bias": dt_bias, "gdn_norm": gdn_norm,
            "w_out": w_out, "norm_mlp": norm_mlp, "w_up": w_up, "w_down": w_down,
            "norm_final": norm_final}


def reference(x, norm_mix, w_in, conv_w, w_pool, pool_scale, a_log, dt_bias, gdn_norm,
              w_out, norm_mlp, w_up, w_down, norm_final):
    P, G, H = POOL_WIDTH, GDN_WIDTH, GDN_HEADS
    h = x
    for l in range(DEPTH):
        u = rms_norm(h, norm_mix[l])
        proj = u @ w_in[l]
        pool_out = pool_mixer(proj[..., :P], w_pool[l], pool_scale[l])
        qkv = causal_conv_silu(proj[..., P:P + 3 * G], conv_w[l])
        z = proj[..., P + 3 * G:P + 4 * G]
        a = proj[..., P + 4 * G:P + 4 * G + H]
        b = proj[..., P + 4 * G + H:P + 4 * G + 2 * H]
        gdn_out = gated_delta_net(qkv[..., :G], qkv[..., G:2 * G], qkv[..., 2 * G:], z, a, b,
                                  a_log[l], dt_bias[l], gdn_norm[l])
        mix = jnp.concatenate([pool_out, gdn_out], axis=-1)
        h = h + mix @ w_out[l]
        u = rms_norm(h, norm_mlp[l])
        h = h + jnp.square(jax.nn.relu(u @ w_up[l])) @ w_down[l]
    return rms_norm(h, norm_final)
```

```python
import numpy as np
from contextlib import ExitStack
import concourse.bass as bass
import concourse.mybir as mybir
from concourse.bass_utils import run_bass_kernel_spmd

F32 = mybir.dt.float32
BF16 = mybir.dt.bfloat16
AF = mybir.ActivationFunctionType
ALU = mybir.AluOpType

ENGS = ("pe", "act", "dve", "pool", "sp")
NRING = 6


class Op:
    __slots__ = ("eng", "fn", "deps", "id", "dma", "signal", "sem", "val", "ndep")

    def __init__(self, eng, fn, dma):
        self.eng = eng
        self.fn = fn
        self.dma = dma
        self.deps = set()
        self.signal = False
        self.sem = None
        self.val = 0


class _Rec:
    def __init__(self):
        self.calls = []

    def __getattr__(self, name):
        def f(*a, **k):
            self.calls.append((name, a, k))
            return self
        return f


class Sched:
    def __init__(self, nc):
        self.nc = nc
        self.ops = []
        self.last_w = {}
        self.readers = {}

    max_ops = None

    def add(self, eng, fn, reads=(), writes=(), dma=False, force=False):
        if self.max_ops is not None and len(self.ops) >= self.max_ops and not force:
            return None
        rec = _Rec()
        fn(rec)
        calls = rec.calls

        def replay(e, calls=calls):
            ins = None
            for (name, a, k) in calls:
                ins = getattr(e, name)(*a, **k)
            return ins
        op = Op(eng, replay, dma)
        op.id = len(self.ops)
        deps = op.deps
        for r in reads:
            w = self.last_w.get(r)
            if w is not None:
                deps.add(w)
            if isinstance(r, tuple) and r and r[0] == "ps":
                for rd in self.readers.get(r, ()):
                    if self.ops[rd].eng != eng:
                        deps.add(rd)
        for k in writes:
            w = self.last_w.get(k)
            if w is not None:
                deps.add(w)
            for rd in self.readers.get(k, ()):
                deps.add(rd)
        for k in writes:
            self.last_w[k] = op.id
            self.readers[k] = []
        ws = set(writes)
        for r in reads:
            if r not in ws:
                self.readers.setdefault(r, []).append(op.id)
        deps.discard(op.id)
        self.ops.append(op)
        return op

    def pe(self, fn, reads=(), writes=()):
        return self.add("pe", fn, reads, writes)

    def act(self, fn, reads=(), writes=()):
        return self.add("act", fn, reads, writes)

    def dve(self, fn, reads=(), writes=()):
        return self.add("dve", fn, reads, writes)

    def pool(self, fn, reads=(), writes=()):
        return self.add("pool", fn, reads, writes)

    def dma(self, q, out, in_, reads=(), writes=()):
        return self.add(q, lambda e: e.dma_start(out=out, in_=in_), reads, writes, dma=True)

    def finalize_and_emit(self, stack):
        nc = self.nc
        ops = self.ops
        for op in ops:
            if op.eng == "pe" and not op.dma:
                op.deps = {d for d in op.deps if not (ops[d].eng == "pe" and not ops[d].dma)}
        for op in ops:
            for d in op.deps:
                ops[d].signal = True
        SEMMAX = 400
        nsig = {e: sum(1 for o in ops if o.eng == e and o.signal and not o.dma) for e in ENGS}
        esem = {e: [stack.enter_context(nc.semaphore(f"s_{e}{i}")) for i in range(nsig[e] // SEMMAX + 1)] for e in ENGS}
        rings = {q: [stack.enter_context(nc.semaphore(f"d_{q}{i}")) for i in range(NRING)] for q in ("sp", "pool", "act")}
        cnt = {e: 0 for e in ENGS}
        dcnt = {q: 0 for q in rings}
        prewait = {}
        for op in ops:
            if op.dma:
                q = op.eng
                i = dcnt[q]
                dcnt[q] += 1
                op.sem = rings[q][i % NRING]
                op.val = 16 * (i // NRING + 1)
                op.signal = True
                if i >= NRING:
                    prewait[op.id] = (op.sem, 16 * (i // NRING))
            elif op.signal:
                op.sem = esem[op.eng][cnt[op.eng] // SEMMAX]
                op.val = cnt[op.eng] % SEMMAX + 1
                cnt[op.eng] += 1
        by_eng = {e: [o for o in ops if o.eng == e] for e in ENGS}
        self.stats = {e: len(by_eng[e]) for e in ENGS}

        def run(ename, eng):
            waited = {}
            for op in by_eng[ename]:
                ws = []
                if op.id in prewait:
                    ws.append(prewait[op.id])
                for d in sorted(op.deps):
                    p = ops[d]
                    ws.append((p.sem, p.val))
                for sem, val in ws:
                    key = id(sem)
                    if waited.get(key, 0) < val:
                        eng.wait_ge(sem, val)
                        waited[key] = val
                ins = op.fn(eng)
                if op.signal and ins is not None:
                    ins.then_inc(op.sem, 16 if op.dma else 1)

        block = stack.enter_context(nc.Block())

        @block.tensor
        def _(e):
            with nc.allow_low_precision("bf16 matmul operands, fp32 accumulate"):
                run("pe", e)

        @block.scalar
        def _(e):
            run("act", e)

        @block.vector
        def _(e):
            run("dve", e)

        @block.gpsimd
        def _(e):
            run("pool", e)

        @block.sync
        def _(e):
            run("sp", e)


T = 2048
NT = 4
D = 1024
EPS = 1e-6
HL = 16
INW = 2568
SA_NMIX = 0
SA_CONV = 8
SA_PSC = 56
SA_INVC = 60
SA_DTB = 124
SA_ALOG = 188
SA_N = 252
C_ID, C_LI, C_US, C_BO, C_NS, C_NIT, C_ON = range(7)
CB_ID, CB_MEAN, CB_ONE, CB_128 = range(4)


def build_A(max_ops=None):
    nc = bass.Bass("TRN2", target_bir_lowering=False)
    dt_ = lambda n, s, k="ExternalInput": nc.dram_tensor(n, s, F32, kind=k).ap()
    hT_d = dt_("hT", [D, T])
    halo_d = dt_("halo", [D, HL])
    win_d = dt_("w_in", [D, INW])
    wpool_d = dt_("w_pool", [4, 128, 128])
    small_d = dt_("smallA", [128, SA_N])
    cst_d = dt_("cstA", [128, 7 * 128])
    cstb_d = dt_("cstAb", [128, 4 * 128])
    mixp_d = dt_("mixp", [512, T], "ExternalOutput")
    sz_d = dt_("sz", [512, T], "ExternalOutput")
    ol_d = dt_("ol", [512, T], "ExternalOutput")
    ep_d = dt_("ep", [512, T], "ExternalOutput")
    spay_d = dt_("spay", [128, 1024], "ExternalOutput")

    with ExitStack() as st:
        sb = lambda n, s, d=F32: st.enter_context(nc.sbuf_tensor("sb_" + n, s, d))
        u = sb("u", [128, 8, HL + T], BF16)
        win = sb("win", [128, 8, INW], BF16)
        wpool = sb("wpool", [128, 4, 128], BF16)
        small = sb("small", [128, SA_N])
        cst = sb("cst", [128, 7 * 128])
        cstb = sb("cstb", [128, 4 * 128], BF16)
        epsc = sb("epsc", [128, 8])
        scr = sb("scr", [128, 3 * (HL + T)])
        hb0 = scr[:, 0:4096].rearrange("p (c n) -> p c n", c=8)
        hbh = sb("hbh", [128, 8, HL])
        sq8 = sb("sq8", [128, 8, 512], BF16)
        rs = [sb(f"rs{i}", [128, 512]) for i in range(2)]
        gt = {n: sb("gt_" + n, [128, 64]) for n in ("beta", "nbeta", "g", "sp", "gc", "gl", "egc", "kbes", "kds", "expA", "egl")}
        eglbc = sb("eglbc", [128, 128])
        dgl = sb("dgl", [128, 128])
        pp = scr[:, 0:HL + T]
        tA = scr[:, HL + T:2 * (HL + T)]
        tB = scr[:, 2 * (HL + T):3 * (HL + T)]
        pooled = sb("pooled", [128, T], BF16)
        tmp16 = sb("tmp16", [128, 16])
        mpo = [sb(f"mpo{i}", [128, 512]) for i in range(2)]
        egcbc = sb("egcbc", [128, T], BF16)
        dg = [sb(f"dg{i}", [128, 128]) for i in range(2)]
        Dg = sb("Dg", [128, 12, 128], BF16)
        xs = [[sb(f"xs{i}_{j}", [128, 3 + 512], BF16) for j in range(3)] for i in range(2)]
        qs = sb("qs", [128, 512])
        ks = sb("ks", [128, 512])
        vs = sb("vs", [128, 512], BF16)
        sqq = sb("sqq", [128, 512], BF16)
        sqk = sb("sqk", [128, 512], BF16)
        rq = sb("rq", [128, 512])
        rk = sb("rk", [128, 512])
        qn = sb("qn", [128, 512], BF16)
        kn = sb("kn", [128, 512], BF16)
        qd = sb("qd", [128, 512], BF16)
        szt = [sb("szt0", [128, 512])] * 2
        olt = [sb("olt0", [128, 512])] * 2
        ept = [sb("ept0", [128, 512])] * 2
        NB = 4
        kbe = [sb(f"kbe{i}", [128, 128], BF16) for i in range(NB)]
        kd = [sb(f"kd{i}", [128, 128], BF16) for i in range(NB)]
        vb = [sb(f"vb{i}", [128, 128], BF16) for i in range(NB)]
        gU = [sb(f"gU{i}", [128, 256]) for i in range(NB)]
        EE = [sb(f"EE{i}", [128, 256]) for i in range(NB)]
        am = [sb(f"am{i}", [128, 128], BF16) for i in range(NB)]
        aiT = [sb(f"aiT{i}", [128, 128], BF16) for i in range(NB)]
        XX = [[sb(f"XX{i}_{k}", [128, 256], BF16) for k in range(2)] for i in range(NB)]
        TT = [[sb(f"TT{i}_{k}", [128, 128], BF16) for k in range(2)] for i in range(NB)]
        valaug = [sb(f"val{i}", [128, 256]) for i in range(NB)]
        kcT = [sb(f"kcT{i}", [128, 128], BF16) for i in range(NB)]
        vnew = [sb(f"vnew{i}", [128, 256], BF16) for i in range(NB)]
        Sf = [sb(f"Sf{i}", [128, 256]) for i in range(4)]
        Sbb = [sb(f"Sbb{i}", [128, 256], BF16) for i in range(4)]
        spay = sb("spay", [128, 4, 256])
        ps = [st.enter_context(nc.psum_tensor(f"ps{i}", [128, 512], F32)) for i in range(8)]
        psb = [p[:].bitcast(BF16) for p in ps]

        s = Sched(nc)
        s.max_ops = max_ops
        psi = [0]

        def nps():
            i = psi[0] % 8
            psi[0] += 1
            return i

        ts = lambda tt: slice(tt * 512, (tt + 1) * 512)
        us = lambda tt: slice(HL + tt * 512, HL + (tt + 1) * 512)
        cf = lambda i: cst[:, i * 128:(i + 1) * 128]
        cb = lambda i: cstb[:, i * 128:(i + 1) * 128]
        P = lambda b: ("ps", b)

        s.dma("sp", small[:], small_d, writes=["small"])
        s.dma("sp", cst[:], cst_d, writes=["cst"])
        s.dma("pool", cstb[:], cstb_d, writes=["cstb"])
        s.dve(lambda e: e.memset(epsc[:, 0:1], EPS), writes=["epsc"])
        s.dve(lambda e: e.memset(epsc[:, 1:2], 128 * EPS), writes=["epsc"])
        s.dve(lambda e: e.memset(epsc[:, 2:3], -1.0), writes=["epsc"])
        for g_ in range(4):
            s.dve(lambda e: e.memset(epsc[:, 3 + g_:4 + g_], 1.0 / (2, 4, 8, 16)[g_]), writes=["epsc"])
        s.dma("sp", hbh[:], halo_d.rearrange("(c p) t -> p c t", p=128), writes=[("hb", 1)])
        s.dma("sp", hb0, hT_d.rearrange("(c p) t -> p c t", p=128)[:, :, ts(0)], writes=[("hb", 0)])
        for cg in range(6):
            c0, c1 = cg * 512, min(INW, (cg + 1) * 512)
            s.dma("pool", win[:, :, c0:c1], win_d.rearrange("(c p) n -> p c n", p=128)[:, :, c0:c1], writes=[("win", cg)])
        s.dma("pool", wpool[:], wpool_d.rearrange("g c d -> c g d"), writes=["wpool"])
        WIN = [("win", cg) for cg in range(6)]

        def norm_tile(tt):
            k = (tt + 1) % 2
            if tt == -1:
                k, n, ucols = 1, HL, slice(0, HL)
            else:
                k, n, ucols = 0, 512, us(tt)
            hbk = hbh if tt == -1 else hb0
            s.act(lambda e: e.activation(out=sq8[:, :, 0:n], in_=hbk[:, :, 0:n], func=AF.Square),
                  reads=[("hb", k)], writes=["sq8"])
            b = nps()
            for kc in range(8):
                s.pe(lambda e, kc=kc: e.matmul(ps[b][:, 0:n], cb(CB_MEAN), sq8[:, kc, 0:n], start=(kc == 0), stop=(kc == 7)),
                     reads=["cstb", "sq8"], writes=[P(b)])
            r = rs[(tt + 1) % 2]
            rk_ = ("rs", (tt + 1) % 2)
            s.act(lambda e: e.activation(out=r[:, 0:n], in_=ps[b][:, 0:n], func=AF.Ln, bias=epsc[:, 0:1]),
                  reads=[P(b), "epsc"], writes=[rk_])
            s.act(lambda e: e.activation(out=r[:, 0:n], in_=r[:, 0:n], func=AF.Exp, scale=-0.5),
                  reads=[rk_], writes=[rk_])
            for kc in range(8):
                s.dve(lambda e, kc=kc: e.scalar_tensor_tensor(out=u[:, kc, ucols], in0=hbk[:, kc, 0:n],
                                                              scalar=small[:, SA_NMIX + kc:SA_NMIX + kc + 1], in1=r[:, 0:n],
                                                              op0=ALU.mult, op1=ALU.mult),
                      reads=[("hb", k), rk_, "small"], writes=[("u", tt, kc)])

        norm_tile(-1)
        for tt in range(NT):
            norm_tile(tt)
            if tt + 1 < NT:
                s.dma("sp", hb0, hT_d.rearrange("(c p) t -> p c t", p=128)[:, :, ts(tt + 1)], writes=[("hb", 0)])
        UALL = [("u", tt) for tt in range(-1, NT)]

        bg = nps()
        for blk in range(16):
            for kc in range(8):
                s.pe(lambda e, blk=blk, kc=kc: e.matmul(ps[bg][:, blk * 8:(blk + 1) * 8], u[:, kc, HL + blk * 128:HL + (blk + 1) * 128],
                                                        win[:, kc, 2560:2568], start=(kc == 0), stop=(kc == 7)),
                     reads=[("u", blk // 4, kc), ("win", 5)], writes=[P(bg)])
        gps = ps[bg][:, 0:128].rearrange("p (b c) -> p b c", c=8)
        v64 = lambda n: gt[n][:].rearrange("p (b h) -> p b h", h=4)
        s.act(lambda e: e.activation(out=v64("beta"), in_=gps[:, :, 4:8], func=AF.Sigmoid), reads=[P(bg)], writes=["g_beta"])
        s.dve(lambda e: e.tensor_scalar(out=gt["nbeta"][:], in0=gt["beta"][:], scalar1=-1.0, scalar2=None, op0=ALU.mult),
              reads=["g_beta"], writes=["g_nbeta"])
        s.dve(lambda e: e.tensor_tensor(out=v64("sp"), in0=gps[:, :, 0:4], in1=small[:, SA_DTB:SA_DTB + 64].rearrange("p (b h) -> p b h", h=4),
                                        op=ALU.add), reads=[P(bg), "small"], writes=["g_sp"])
        s.act(lambda e: e.activation(out=gt["sp"][:], in_=gt["sp"][:], func=AF.Exp), reads=["g_sp"], writes=["g_sp"])
        s.dve(lambda e: e.tensor_scalar(out=gt["sp"][:], in0=gt["sp"][:], scalar1=1.0, scalar2=None, op0=ALU.add),
              reads=["g_sp"], writes=["g_sp"])
        s.act(lambda e: e.activation(out=gt["sp"][:], in_=gt["sp"][:], func=AF.Ln), reads=["g_sp"], writes=["g_sp"])
        s.act(lambda e: e.activation(out=gt["expA"][:], in_=small[:, SA_ALOG:SA_ALOG + 64], func=AF.Exp),
              reads=["small"], writes=["g_expA"])
        s.dve(lambda e: e.scalar_tensor_tensor(out=gt["g"][:], in0=gt["sp"][:], scalar=epsc[:, 2:3], in1=gt["expA"][:], op0=ALU.mult, op1=ALU.mult),
              reads=["g_sp", "g_expA", "epsc"], writes=["g_g"])
        b1 = nps()
        s.pe(lambda e: e.matmul(ps[b1][:, 0:64], cf(C_LI), gt["g"][:], start=True, stop=True), reads=["cst", "g_g"], writes=[P(b1)])
        s.pe(lambda e: e.matmul(ps[b1][:, 64:128], cf(C_BO), gt["g"][:], start=True, stop=True), reads=["cst", "g_g"], writes=[P(b1)])
        s.dve(lambda e: e.tensor_copy(out=gt["gc"][:], in_=ps[b1][:, 0:64]), reads=[P(b1)], writes=["g_gc"])
        s.dve(lambda e: e.tensor_copy(out=gt["gl"][:], in_=ps[b1][:, 64:128]), reads=[P(b1)], writes=["g_gl"])
        s.act(lambda e: e.activation(out=gt["egc"][:], in_=gt["gc"][:], func=AF.Exp), reads=["g_gc"], writes=["g_egc"])
        s.act(lambda e: e.activation(out=gt["egl"][:], in_=gt["gl"][:], func=AF.Exp), reads=["g_gl"], writes=["g_egl"])
        s.dve(lambda e: e.tensor_tensor(out=gt["kds"][:], in0=gt["gl"][:], in1=gt["gc"][:], op=ALU.subtract),
              reads=["g_gl", "g_gc"], writes=["g_kds"])
        s.act(lambda e: e.activation(out=gt["kds"][:], in_=gt["kds"][:], func=AF.Exp), reads=["g_kds"], writes=["g_kds"])
        s.dve(lambda e: e.tensor_tensor(out=gt["kbes"][:], in0=gt["beta"][:], in1=gt["egc"][:], op=ALU.mult),
              reads=["g_beta", "g_egc"], writes=["g_kbes"])
        for n in range(2):
            s.dve(lambda e: e.tensor_scalar(out=dgl[:, n * 64:(n + 1) * 64], in0=gt["egl"][:], scalar1=cst[:, C_ID * 128 + 64 * n:C_ID * 128 + 64 * n + 1],
                                            scalar2=None, op0=ALU.mult),
                  reads=["g_egl", "cst"], writes=[("dgl", n)])
        b2 = nps()
        s.pe(lambda e: e.matmul(ps[b2][:, 0:128], cf(C_ON), dgl[:], start=True, stop=True), reads=["cst", ("dgl", 0), ("dgl", 1)], writes=[P(b2)])
        s.dve(lambda e: e.tensor_copy(out=eglbc[:], in_=ps[b2][:, 0:128]), reads=[P(b2)], writes=["eglbc"])
        def inproj(col0, tt, b, ncols=128):
            if tt == -1:
                ucols, n = slice(0, HL), HL
            else:
                ucols, n = us(tt), 512
            for kc in range(8):
                s.pe(lambda e, kc=kc: e.matmul(ps[b][0:ncols, 0:n], win[:, kc, col0:col0 + ncols], u[:, kc, ucols], start=(kc == 0), stop=(kc == 7)),
                     reads=[("u", tt, kc), ("win", col0 // 512), ("win", (col0 + ncols - 1) // 512)], writes=[P(b)])
            return n

        mi = 0
        for g in range(4):
            w = (2, 4, 8, 16)[g]
            for tt in range(-1, NT):
                b = nps()
                n = inproj(g * 128, tt, b)
                dst = pp[:, 0:HL] if tt == -1 else pp[:, us(tt)]
                eng = "act" if tt % 2 else "dve"
                if eng == "act":
                    s.act(lambda e, b=b, n=n, dst=dst: e.copy(out=dst, in_=ps[b][:, 0:n]), reads=[P(b)], writes=[("pp", tt), ("hb", 0)])
                else:
                    s.dve(lambda e, b=b, n=n, dst=dst: e.tensor_copy(out=dst, in_=ps[b][:, 0:n]), reads=[P(b)], writes=[("pp", tt), ("hb", 0)])
            PP = [("pp", tt) for tt in range(-1, NT)]
            L = HL + T
            srcs = [(pp, PP)]
            steps = [(tA, 1), (tB, 2), (tA, 4), (tB, 8)][:g + 1]
            cur, curk = pp, PP
            lo = 0
            for (dstb, sh) in steps:
                lo2 = lo + sh
                dk = "tA" if dstb is tA else "tB"
                s.dve(lambda e, dstb=dstb, cur=cur, lo2=lo2, sh=sh: e.tensor_tensor(out=dstb[:, lo2:L], in0=cur[:, lo2:L], in1=cur[:, lo2 - sh:L - sh], op=ALU.add),
                       reads=list(curk), writes=[dk, ("hb", 0)])
                cur, curk, lo = dstb, [dk], lo2
            s.dve(lambda e, cur=cur, w=w: e.scalar_tensor_tensor(out=pooled[:], in0=cur[:, HL:L], scalar=epsc[:, 3 + g:4 + g], in1=pp[:, HL:L],
                                                                 op0=ALU.mult, op1=ALU.subtract),
                  reads=list(curk) + PP + ["epsc"], writes=["pooled"])
            s.dve(lambda e, cur=cur, g=g: e.tensor_tensor(out=tmp16[:], in0=cur[:, HL:HL + 16], in1=small[:, SA_INVC + g * 16:SA_INVC + (g + 1) * 16], op=ALU.mult),
                  reads=list(curk) + ["small"], writes=["tmp16"])
            s.dve(lambda e: e.tensor_tensor(out=pooled[:, 0:16], in0=tmp16[:], in1=pp[:, HL:HL + 16], op=ALU.subtract),
                  reads=["tmp16"] + PP, writes=["pooled"])
            for tt in range(NT):
                b = nps()
                s.pe(lambda e, b=b, g=g, tt=tt: e.matmul(ps[b][:], wpool[:, g, :], pooled[:, ts(tt)], start=True, stop=True),
                     reads=["wpool", "pooled"], writes=[P(b)])
                m = mi % 2
                mi += 1
                s.act(lambda e, b=b, m=m, g=g: e.activation(out=mpo[m][:], in_=ps[b][:], func=AF.Identity, scale=small[:, SA_PSC + g:SA_PSC + g + 1]),
                      reads=[P(b), "small"], writes=[("mpo", m)])
                s.dma("sp", mixp_d[g * 128:(g + 1) * 128, ts(tt)], mpo[m][:], reads=[("mpo", m)], writes=[("o_mixp", g, tt)])
        outk = [("o_mixp", g, tt) for g in range(4) for tt in range(NT)]
        it = 0
        for hh in range(4):
            cq, ck, cv, cz = 512 + hh * 128, 1024 + hh * 128, 1536 + hh * 128, 2048 + hh * 128
            s.dve(lambda e, hh=hh: e.memset(Sf[hh][:, 0:128], 0.0), writes=[("Sf", hh)])
            s.dve(lambda e, hh=hh: e.tensor_copy(out=Sf[hh][:, 128:256], in_=cf(C_ID)), reads=["cst"], writes=[("Sf", hh)])
            s.act(lambda e, hh=hh: e.copy(out=Sbb[hh][:], in_=Sf[hh][:]), reads=[("Sf", hh)], writes=[("Sbb", hh)])
            for j in range(3):
                for tap in range(4):
                    col = SA_CONV + (j * 4 + hh) * 4 + tap
                    s.dve(lambda e, j=j, tap=tap, col=col: e.tensor_scalar(out=Dg[:, j * 4 + tap, :], in0=cf(C_ID), scalar1=small[:, col:col + 1],
                                                                          scalar2=None, op0=ALU.mult),
                          reads=["cst", "small"], writes=[("Dg", j * 4 + tap)])
            for blk in range(16):
                d = blk % 2
                s.dve(lambda e, d=d, blk=blk, hh=hh: e.tensor_scalar(out=dg[d][:], in0=cf(C_ID), scalar1=gt["egc"][:, blk * 4 + hh:blk * 4 + hh + 1],
                                                                    scalar2=None, op0=ALU.mult),
                      reads=["cst", "g_egc"], writes=[("dg", d)])
                b = nps()
                s.pe(lambda e, b=b, d=d: e.matmul(ps[b][:, 0:128], cf(C_ON), dg[d][:], start=True, stop=True), reads=["cst", ("dg", d)], writes=[P(b)])
                s.act(lambda e, b=b, blk=blk: e.copy(out=egcbc[:, blk * 128:(blk + 1) * 128], in_=ps[b][:, 0:128]), reads=[P(b)], writes=[("egcbc", blk // 4)])

            for tt in range(NT):
                x = xs[it % 2]
                xprev = xs[(it + 1) % 2]
                xk = it % 2
                zi = it % 2
                it += 1
                for j, c0 in enumerate((cq, ck, cv)):
                    if tt == 0:
                        b = nps()
                        inproj(c0, -1, b)
                        s.act(lambda e, b=b, j=j, x=x: e.copy(out=x[j][:, 0:3], in_=ps[b][:, HL - 3:HL]), reads=[P(b)], writes=[("xs", xk, j)])
                    else:
                        s.act(lambda e, j=j, x=x, xprev=xprev: e.copy(out=x[j][:, 0:3], in_=xprev[j][:, 512:515]),
                              reads=[("xs", 1 - xk, j)], writes=[("xs", xk, j)])
                    b = nps()
                    inproj(c0, tt, b)
                    if j == 1:
                        s.dve(lambda e, b=b, j=j, x=x: e.tensor_copy(out=x[j][:, 3:515], in_=ps[b][:]), reads=[P(b)], writes=[("xs", xk, j)])
                    else:
                        s.act(lambda e, b=b, j=j, x=x: e.copy(out=x[j][:, 3:515], in_=ps[b][:]), reads=[P(b)], writes=[("xs", xk, j)])
                b = nps()
                inproj(cz, tt, b)
                s.act(lambda e, b=b, zi=zi: e.activation(out=szt[zi][:], in_=ps[b][:], func=AF.Silu), reads=[P(b)], writes=[("szt", 0)])
                s.dma("sp", sz_d[hh * 128:(hh + 1) * 128, ts(tt)], szt[zi][:], reads=[("szt", 0)], writes=[("o_sz", hh, tt)])
                outk.append(("o_sz", hh, tt))
                for j, dst, dk in ((0, qs, "qs"), (1, ks, "ks"), (2, vs, "vs")):
                    b = nps()
                    for tap in range(4):
                        s.pe(lambda e, b=b, j=j, tap=tap, x=x: e.matmul(ps[b][:], Dg[:, j * 4 + tap, :], x[j][:, tap:tap + 512], start=(tap == 0), stop=(tap == 3)),
                             reads=[("Dg", j * 4 + tap), ("xs", xk, j)], writes=[P(b)])
                    s.act(lambda e, b=b, dst=dst: e.activation(out=dst[:], in_=ps[b][:], func=AF.Silu), reads=[P(b)], writes=[dk])
                s.act(lambda e: e.activation(out=sqq[:], in_=qs[:], func=AF.Square), reads=["qs"], writes=["sqq"])
                s.act(lambda e: e.activation(out=sqk[:], in_=ks[:], func=AF.Square), reads=["ks"], writes=["sqk"])
                bq, bk = nps(), nps()
                s.pe(lambda e, bq=bq: e.matmul(ps[bq][:], cb(CB_128), sqq[:], start=True, stop=True), reads=["cstb", "sqq"], writes=[P(bq)])
                s.pe(lambda e, bk=bk: e.matmul(ps[bk][:], cb(CB_ONE), sqk[:], start=True, stop=True), reads=["cstb", "sqk"], writes=[P(bk)])
                s.act(lambda e, bq=bq: e.activation(out=rq[:], in_=ps[bq][:], func=AF.Ln, bias=epsc[:, 1:2]), reads=[P(bq), "epsc"], writes=["rq"])
                s.act(lambda e, bk=bk: e.activation(out=rk[:], in_=ps[bk][:], func=AF.Ln, bias=epsc[:, 0:1]), reads=[P(bk), "epsc"], writes=["rk"])
                s.act(lambda e: e.activation(out=rq[:], in_=rq[:], func=AF.Exp, scale=-0.5), reads=["rq"], writes=["rq"])
                s.act(lambda e: e.activation(out=rk[:], in_=rk[:], func=AF.Exp, scale=-0.5), reads=["rk"], writes=["rk"])
                s.dve(lambda e: e.tensor_tensor(out=qn[:], in0=qs[:], in1=rq[:], op=ALU.mult), reads=["qs", "rq"], writes=["qn"])
                s.dve(lambda e: e.tensor_tensor(out=kn[:], in0=ks[:], in1=rk[:], op=ALU.mult), reads=["ks", "rk"], writes=["kn"])
                s.dve(lambda e, tt=tt: e.tensor_tensor(out=qd[:], in0=qn[:], in1=egcbc[:, ts(tt)], op=ALU.mult),
                       reads=["qn", ("egcbc", tt)], writes=["qd"])
                psi[0] = 0
                BL = range(4)
                cbk = lambda bl: slice(bl * 128, (bl + 1) * 128)
                gcol = lambda bl: (tt * 4 + bl) * 4 + hh
                for bl in BL:
                    A, B = bl, 4 + bl
                    s.pe(lambda e, bl=bl, B=B: e.transpose(psb[B][:, 0:128], kn[:, cbk(bl)], cb(CB_ID)), reads=["kn", "cstb"], writes=[P(B)])
                    s.pe(lambda e, bl=bl, B=B: e.transpose(psb[B][:, 128:256], vs[:, cbk(bl)], cb(CB_ID)), reads=["vs", "cstb"], writes=[P(B)])
                    s.pe(lambda e, bl=bl, A=A: e.matmul(ps[A][:, 0:128], kn[:, cbk(bl)], kn[:, cbk(bl)], start=True, stop=True), reads=["kn"], writes=[P(A)])
                    s.pe(lambda e, bl=bl, A=A: e.matmul(ps[A][:, 384:512], kn[:, cbk(bl)], qn[:, cbk(bl)], start=True, stop=True), reads=["kn", "qn"], writes=[P(A)])
                for bl in BL:
                    B = 4 + bl
                    gc_ = gcol(bl)
                    s.act(lambda e, bl=bl, B=B, gc_=gc_: e.activation(out=kbe[bl][:], in_=psb[B][:, 0:128], func=AF.Identity, scale=gt["kbes"][:, gc_:gc_ + 1]),
                          reads=[P(B), "g_kbes"], writes=[("kbe", bl)])
                    s.dve(lambda e, bl=bl, B=B, gc_=gc_: e.tensor_scalar(out=kd[bl][:], in0=psb[B][:, 0:128], scalar1=gt["kds"][:, gc_:gc_ + 1], scalar2=None, op0=ALU.mult),
                          reads=[P(B), "g_kds"], writes=[("kd", bl)])
                    s.act(lambda e, bl=bl, B=B, gc_=gc_: e.activation(out=vb[bl][:], in_=psb[B][:, 128:256], func=AF.Identity, scale=gt["beta"][:, gc_:gc_ + 1]),
                          reads=[P(B), "g_beta"], writes=[("vb", bl)])
                    s.dve(lambda e, bl=bl, gc_=gc_: e.tensor_scalar(out=gU[bl][:, 0:128], in0=cf(C_US), scalar1=gt["g"][:, gc_:gc_ + 1], scalar2=None, op0=ALU.mult),
                          reads=["cst", "g_g"], writes=[("gU", bl)])
                    s.dve(lambda e, bl=bl, gc_=gc_: e.tensor_scalar(out=gU[bl][:, 128:256], in0=cf(C_LI), scalar1=gt["g"][:, gc_:gc_ + 1], scalar2=None, op0=ALU.mult),
                          reads=["cst", "g_g"], writes=[("gU", bl)])
                for bl in BL:
                    A = bl
                    s.pe(lambda e, bl=bl, A=A: e.matmul(ps[A][:, 128:256], cf(C_LI), gU[bl][:, 0:128], start=True, stop=False), reads=["cst", ("gU", bl)], writes=[P(A)])
                    s.pe(lambda e, bl=bl, A=A: e.matmul(ps[A][:, 128:256], cf(C_ID), cf(C_NS), start=False, stop=True), reads=["cst"], writes=[P(A)])
                    s.pe(lambda e, bl=bl, A=A: e.matmul(ps[A][:, 256:384], cf(C_US), gU[bl][:, 128:256], start=True, stop=False), reads=["cst", ("gU", bl)], writes=[P(A)])
                    s.pe(lambda e, bl=bl, A=A: e.matmul(ps[A][:, 256:384], cf(C_ID), cf(C_NIT), start=False, stop=True), reads=["cst"], writes=[P(A)])
                for bl in BL:
                    A = bl
                    gc_ = gcol(bl)
                    s.act(lambda e, bl=bl, A=A: e.activation(out=EE[bl][:], in_=ps[A][:, 128:384], func=AF.Exp), reads=[P(A)], writes=[("EE", bl)])
                    s.dve(lambda e, bl=bl, A=A, gc_=gc_: e.scalar_tensor_tensor(out=am[bl][:], in0=ps[A][:, 0:128], scalar=gt["nbeta"][:, gc_:gc_ + 1],
                                                                              in1=EE[bl][:, 0:128], op0=ALU.mult, op1=ALU.mult),
                          reads=[P(A), ("EE", bl), "g_nbeta"], writes=[("am", bl)])
                    s.dve(lambda e, bl=bl, A=A: e.tensor_tensor(out=aiT[bl][:], in0=ps[A][:, 384:512], in1=EE[bl][:, 128:256], op=ALU.mult),
                          reads=[P(A), ("EE", bl)], writes=[("aiT", bl)])
                for bl in BL:
                    B = 4 + bl
                    s.pe(lambda e, bl=bl, B=B: e.transpose(psb[B][:, 0:128], am[bl][:], cb(CB_ID)), reads=[("am", bl), "cstb"], writes=[P(B)])
                for bl in BL:
                    B = 4 + bl
                    s.dve(lambda e, bl=bl: e.tensor_copy(out=XX[bl][0][:, 0:128], in_=am[bl][:]), reads=[("am", bl)], writes=[("XX", bl, 0)])
                    s.act(lambda e, bl=bl, B=B: e.copy(out=XX[bl][0][:, 128:256], in_=psb[B][:, 0:128]), reads=[P(B)], writes=[("XX", bl, 0)])
                    s.dve(lambda e, bl=bl, B=B: e.tensor_tensor(out=TT[bl][0][:], in0=psb[B][:, 0:128], in1=cb(CB_ID), op=ALU.add),
                          reads=[P(B), "cstb"], writes=[("TT", bl, 0)])
                for k in range(5):
                    k0, k1 = k % 2, (k + 1) % 2
                    for bl in BL:
                        B = 4 + bl
                        X, XT = XX[bl][k0][:, 0:128], XX[bl][k0][:, 128:256]
                        s.pe(lambda e, B=B, X=X, XT=XT: e.matmul(ps[B][:, 0:128], XT, X, start=True, stop=True), reads=[("XX", bl, k0)], writes=[P(B)])
                        s.pe(lambda e, B=B, X=X, XT=XT: e.matmul(ps[B][:, 128:256], X, XT, start=True, stop=True), reads=[("XX", bl, k0)], writes=[P(B)])
                    for bl in BL:
                        B = 4 + bl
                        s.act(lambda e, bl=bl, B=B, k1=k1: e.copy(out=XX[bl][k1][:], in_=ps[B][:, 0:256]), reads=[P(B)], writes=[("XX", bl, k1)])
                    for bl in BL:
                        B = 4 + bl
                        s.pe(lambda e, bl=bl, B=B, k0=k0: e.matmul(ps[B][:, 256:384], cb(CB_ID), TT[bl][k0][:], start=True, stop=False),
                             reads=["cstb", ("TT", bl, k0)], writes=[P(B)])
                        s.pe(lambda e, bl=bl, B=B, k0=k0, k1=k1: e.matmul(ps[B][:, 256:384], XX[bl][k1][:, 0:128], TT[bl][k0][:], start=False, stop=True),
                             reads=[("XX", bl, k1), ("TT", bl, k0)], writes=[P(B)])
                    for bl in BL:
                        B = 4 + bl
                        s.dve(lambda e, bl=bl, B=B, k1=k1: e.tensor_copy(out=TT[bl][k1][:], in_=ps[B][:, 256:384]), reads=[P(B)], writes=[("TT", bl, k1)])
                TF = 1
                for bl in BL:
                    B = 4 + bl
                    s.pe(lambda e, bl=bl, B=B: e.matmul(ps[B][:, 0:128], TT[bl][TF][:], vb[bl][:], start=True, stop=True), reads=[("TT", bl, TF), ("vb", bl)], writes=[P(B)])
                    s.pe(lambda e, bl=bl, B=B: e.matmul(ps[B][:, 128:256], kbe[bl][:], TT[bl][TF][:], start=True, stop=True), reads=[("TT", bl, TF), ("kbe", bl)], writes=[P(B)])
                for bl in BL:
                    B = 4 + bl
                    if it == 1:
                        s.dve(lambda e, bl=bl: e.memset(valaug[bl][:, 128:256], 0.0), writes=[("val", bl)])
                    s.dve(lambda e, bl=bl, B=B: e.tensor_copy(out=valaug[bl][:, 0:128], in_=ps[B][:, 0:128]), reads=[P(B)], writes=[("val", bl)])
                    s.act(lambda e, bl=bl, B=B: e.copy(out=kcT[bl][:], in_=ps[B][:, 128:256]), reads=[P(B)], writes=[("kcT", bl)])
                oi = zi
                for bl in BL:
                    A, B = bl, 4 + bl
                    for n in range(2):
                        r = slice(64 * n, 64 * n + 64)
                        cc = slice(bl * 128 + 64 * n, bl * 128 + 64 * n + 64)
                        ecol = n * 64 + (tt * 4 + bl) * 4 + hh
                        s.pe(lambda e, bl=bl, A=A: e.matmul(ps[A][:, 0:256], kcT[bl][:], Sbb[hh][:], start=True, stop=True),
                             reads=[("kcT", bl), ("Sbb", hh)], writes=[P(A)])
                        s.dve(lambda e, bl=bl, A=A, r=r: e.tensor_tensor(out=vnew[bl][r, :], in0=valaug[bl][r, :], in1=ps[A][r, 0:256], op=ALU.subtract),
                              reads=[P(A), ("val", bl)], writes=[("vnew", bl)])
                        for m in range(2):
                            oc = slice(m * 128 + 64 * n, m * 128 + 64 * n + 64)
                            s.pe(lambda e, bl=bl, B=B, m=m, oc=oc, cc=cc: e.matmul(ps[B][:, oc], Sbb[hh][:, m * 128:(m + 1) * 128], qd[:, cc], start=True, stop=False),
                                 reads=[("Sbb", hh), "qd"], writes=[P(B)])
                            s.pe(lambda e, bl=bl, B=B, m=m, oc=oc, r=r: e.matmul(ps[B][:, oc], vnew[bl][r, m * 128:(m + 1) * 128], aiT[bl][r, r], start=False, stop=True),
                                 reads=[("vnew", bl), ("aiT", bl)], writes=[P(B)])
                        s.pe(lambda e, bl=bl, A=A, r=r: e.matmul(ps[A][:, 256:512], kd[bl][r, :], vnew[bl][r, :], start=True, stop=True),
                             reads=[("kd", bl), ("vnew", bl)], writes=[P(A)])
                        s.dve(lambda e, A=A, ecol=ecol: e.scalar_tensor_tensor(out=Sbb[hh][:], in0=Sf[hh][:], scalar=eglbc[:, ecol:ecol + 1], in1=ps[A][:, 256:512],
                                                                              op0=ALU.mult, op1=ALU.add),
                              reads=[("Sf", hh), "eglbc", P(A)], writes=[("Sbb", hh)])
                        s.dve(lambda e, A=A, ecol=ecol: e.scalar_tensor_tensor(out=Sf[hh][:], in0=Sf[hh][:], scalar=eglbc[:, ecol:ecol + 1], in1=ps[A][:, 256:512],
                                                                              op0=ALU.mult, op1=ALU.add),
                              reads=[("Sf", hh), "eglbc", P(A)], writes=[("Sf", hh)])
                    s.act(lambda e, bl=bl, B=B, oi=oi: e.copy(out=olt[oi][:, cbk(bl)], in_=ps[B][:, 0:128]), reads=[P(B)], writes=[("olt", 0)])
                    s.act(lambda e, bl=bl, B=B, oi=oi: e.copy(out=ept[oi][:, cbk(bl)], in_=ps[B][:, 128:256]), reads=[P(B)], writes=[("ept", 0)])
                s.dma("sp", ol_d[hh * 128:(hh + 1) * 128, ts(tt)], olt[oi][:], reads=[("olt", 0)], writes=[("o_ol", hh, tt)])
                s.dma("sp", ep_d[hh * 128:(hh + 1) * 128, ts(tt)], ept[oi][:], reads=[("ept", 0)], writes=[("o_ep", hh, tt)])
                outk += [("o_ol", hh, tt), ("o_ep", hh, tt)]
            b = nps()
            s.pe(lambda e, b=b, hh=hh: e.transpose(ps[b][:, 0:128], Sf[hh][:, 128:256], cf(C_ID)), reads=[("Sf", hh), "cst"], writes=[P(b)])
            s.dve(lambda e, hh=hh: e.tensor_copy(out=spay[:, hh, 0:128], in_=Sf[hh][:, 0:128]), reads=[("Sf", hh)], writes=["spay"])
            s.dve(lambda e, b=b, hh=hh: e.tensor_copy(out=spay[:, hh, 128:256], in_=ps[b][:, 0:128]), reads=[P(b)], writes=["spay"])
        s.dma("sp", spay_d, spay[:].rearrange("p h n -> p (h n)"), reads=["spay"], writes=["o_spay"])
        outk.append("o_spay")
        s.add("sp", lambda e: None, reads=outk, force=True)
        s.finalize_and_emit(st)
    return nc


T = 2048
NT = 4
D = 1024
EPS = 1e-6
SB_CMASK = 0
SB_GN = 3
SB_NMLP = 4
SB_NFIN = 12
SB_N = 20


def build_B(last):
    nc = bass.Bass("TRN2", target_bir_lowering=False)
    dt_ = lambda n, s, k="ExternalInput": nc.dram_tensor(n, s, F32, kind=k).ap()
    hT_d = dt_("hT", [D, T])
    mixp_d = dt_("mixp", [512, T])
    ol_d = dt_("ol", [512, T])
    ep_d = dt_("ep", [512, T])
    sz_d = dt_("sz", [512, T])
    pay_d = dt_("pay", [3, 128, 1024])
    small_d = dt_("smallB", [128, SB_N])
    wout_d = dt_("w_out", [D, D])
    wup_d = dt_("w_up", [D, 4 * D])
    wdn_d = dt_("w_down", [4 * D, D])
    cst_d = dt_("cstB", [128, 256])
    hn_d = dt_("hn", [D, T], "ExternalOutput")
    on_d = dt_("on", [D, T], "ExternalOutput") if last else None

    with ExitStack() as st:
        sb = lambda n, s, d=F32: st.enter_context(nc.sbuf_tensor("sb_" + n, s, d))
        h = sb("h", [128, 8, T])
        u = sb("u", [128, 8, T], BF16)
        mix = u
        small = sb("small", [128, SB_N])
        cst = sb("cst", [128, 256], BF16)
        scr = sb("scr", [128, 4096])
        S = sb("S", [128, 4, 128])
        Sb = sb("Sb", [128, 4, 128], BF16)
        tmpS = sb("tmpS", [128, 4, 128])
        epsc = sb("epsc", [128, 1])
        wout = sb("wout", [128, 8, D], BF16)
        wu = [sb(f"wu{i}", [128, 8, 512], BF16) for i in range(2)]
        wd = [sb(f"wd{i}", [128, 4, D], BF16) for i in range(2)]
        epb = [sb(f"epb{i}", [128, 512], BF16) for i in range(2)]
        olb = [sb(f"olb{i}", [128, 512]) for i in range(2)]
        szb = [sb(f"szb{i}", [128, 512]) for i in range(2)]
        ob = [sb(f"ob{i}", [128, 512]) for i in range(2)]
        sq1 = [sb(f"sq1{i}", [128, 512], BF16) for i in range(2)]
        rs = [sb(f"rs{i}", [128, 512]) for i in range(2)]
        onb = [sb(f"onb{i}", [128, 512]) for i in range(2)]
        sq8 = [scr[:, i * 2048:(i + 1) * 2048].bitcast(BF16).rearrange("p (c n) -> p c n", c=8) for i in range(2)]
        rl = [sb(f"rl{i}", [128, 512]) for i in range(3)]
        actb = [sb(f"actb{i}", [128, 4, 512], BF16) for i in range(2)]
        ot = [sb(f"ot{i}", [128, 512]) for i in range(2)] if last else None
        ps = [st.enter_context(nc.psum_tensor(f"ps{i}", [128, 512], F32)) for i in range(8)]

        s = Sched(nc)
        psi = [0]

        def nps():
            i = psi[0] % 8
            psi[0] += 1
            return i

        ts = lambda tt: slice(tt * 512, (tt + 1) * 512)

        s.dma("sp", small[:], small_d, writes=["small"])
        s.dma("pool", cst[:], cst_d, writes=["cst"])
        s.dma("sp", scr[:, 0:3072].rearrange("p (j n) -> p j n", j=3), pay_d.rearrange("j p n -> p j n"), writes=["pay", ("sq8", 0), ("sq8", 1)])
        for tt in range(NT):
            s.dma("sp", h[:, :, ts(tt)], hT_d.rearrange("(c p) t -> p c t", p=128)[:, :, ts(tt)],
                  writes=[("h", c, tt) for c in range(8)])
        s.dma("pool", wout[:], wout_d.rearrange("(c p) n -> p c n", p=128), writes=["wout"])
        s.dve(lambda e: e.memset(epsc[:], EPS), writes=["epsc"])
        s.dve(lambda e: e.memset(S[:], 0.0), writes=["S"])
        onesm1024 = cst[:, 0:128]
        onesm128 = cst[:, 128:256]

        payv = scr[:, 0:3072].rearrange("p (j h n) -> p j h n", j=3, h=4)
        for j in range(3):
            b = nps()
            for hh in range(4):
                s.pe(lambda e, b=b, hh=hh, j=j: e.matmul(ps[b][:, hh * 128:(hh + 1) * 128], payv[:, j, hh, 128:256], S[:, hh, :],
                                                         start=True, stop=True),
                     reads=["pay", "S", ("sq8", 0), ("sq8", 1)], writes=[("ps", b)])
            s.dve(lambda e, b=b, j=j: e.tensor_tensor(out=tmpS[:], in0=ps[b][:].rearrange("p (h n) -> p h n", h=4),
                                                      in1=payv[:, j, :, 0:128], op=ALU.add),
                  reads=[("ps", b), "pay", ("sq8", 0), ("sq8", 1)], writes=["tmpS"])
            s.dve(lambda e: e.tensor_tensor(out=tmpS[:], in0=tmpS[:], in1=S[:], op=ALU.subtract),
                  reads=["tmpS", "S"], writes=["tmpS"])
            s.dve(lambda e, j=j: e.scalar_tensor_tensor(out=S[:], in0=tmpS[:], scalar=small[:, SB_CMASK + j:SB_CMASK + j + 1],
                                                        in1=S[:], op0=ALU.mult, op1=ALU.add),
                  reads=["tmpS", "S", "small"], writes=["S"])
        s.act(lambda e: e.copy(out=Sb[:], in_=S[:]), reads=["S"], writes=["Sb"])

        for tt in range(NT):
            s.dma("pool", mix[:, 0:4, ts(tt)], mixp_d.rearrange("(c p) t -> p c t", p=128)[:, :, ts(tt)],
                  writes=[("u", c, tt) for c in range(4)])
        it = 0
        for hh in range(4):
            for tt in range(NT):
                k = it % 2
                it += 1
                rows = slice(hh * 128, (hh + 1) * 128)
                s.dma("pool", epb[k][:], ep_d[rows, ts(tt)], writes=[("epb", k)])
                s.dma("sp", olb[k][:], ol_d[rows, ts(tt)], writes=[("olb", k)])
                s.dma("sp", szb[k][:], sz_d[rows, ts(tt)], writes=[("szb", k)])
                b = nps()
                s.pe(lambda e, b=b, k=k, hh=hh: e.matmul(ps[b][:], Sb[:, hh, :], epb[k][:], start=True, stop=True),
                     reads=["Sb", ("epb", k)], writes=[("ps", b)])
                s.dve(lambda e, b=b, k=k: e.tensor_tensor(out=ob[k][:], in0=olb[k][:], in1=ps[b][:], op=ALU.add),
                      reads=[("ps", b), ("olb", k)], writes=[("ob", k)])
                s.act(lambda e, k=k: e.activation(out=sq1[k][:], in_=ob[k][:], func=AF.Square),
                      reads=[("ob", k)], writes=[("sq1", k)])
                b2 = nps()
                s.pe(lambda e, b2=b2, k=k: e.matmul(ps[b2][:], onesm128, sq1[k][:], start=True, stop=True),
                     reads=["cst", ("sq1", k)], writes=[("ps", b2)])
                s.act(lambda e, b2=b2, k=k: e.activation(out=rs[k][:], in_=ps[b2][:], func=AF.Ln, bias=epsc[:, 0:1]),
                      reads=[("ps", b2), "epsc"], writes=[("rs", k)])
                s.act(lambda e, k=k: e.activation(out=rs[k][:], in_=rs[k][:], func=AF.Exp, scale=-0.5),
                      reads=[("rs", k)], writes=[("rs", k)])
                s.dve(lambda e, k=k: e.scalar_tensor_tensor(out=onb[k][:], in0=ob[k][:], scalar=small[:, SB_GN:SB_GN + 1],
                                                            in1=rs[k][:], op0=ALU.mult, op1=ALU.mult),
                      reads=[("ob", k), ("rs", k), "small"], writes=[("onb", k)])
                s.pool(lambda e, k=k, hh=hh, tt=tt: e.tensor_tensor(out=mix[:, 4 + hh, ts(tt)], in0=onb[k][:], in1=szb[k][:], op=ALU.mult),
                       reads=[("onb", k), ("szb", k)], writes=[("u", 4 + hh, tt)])

        for tt in range(NT):
            for dc in range(8):
                b = nps()
                for kc in range(8):
                    s.pe(lambda e, b=b, kc=kc, dc=dc, tt=tt: e.matmul(ps[b][:], wout[:, kc, dc * 128:(dc + 1) * 128], mix[:, kc, ts(tt)],
                                                                     start=(kc == 0), stop=(kc == 7)),
                         reads=["wout", ("u", kc, tt)], writes=[("ps", b)])
                s.dve(lambda e, b=b, dc=dc, tt=tt: e.tensor_tensor(out=h[:, dc, ts(tt)], in0=h[:, dc, ts(tt)], in1=ps[b][:], op=ALU.add),
                      reads=[("ps", b), ("h", dc, tt)], writes=[("h", dc, tt)])

        def rmsnorm(tt, k, wcol, dst_fn, dst_keys, eng_alt):
            s.act(lambda e, k=k, tt=tt: e.activation(out=sq8[k], in_=h[:, :, ts(tt)], func=AF.Square),
                  reads=[("h", c, tt) for c in range(8)], writes=[("sq8", k)])
            b = nps()
            for kc in range(8):
                s.pe(lambda e, b=b, k=k, kc=kc: e.matmul(ps[b][:], onesm1024, sq8[k][:, kc, :], start=(kc == 0), stop=(kc == 7)),
                     reads=["cst", ("sq8", k)], writes=[("ps", b)])
            s.act(lambda e, b=b, k=k: e.activation(out=rs[k][:], in_=ps[b][:], func=AF.Ln, bias=epsc[:, 0:1]),
                  reads=[("ps", b), "epsc"], writes=[("rs", k)])
            s.act(lambda e, k=k: e.activation(out=rs[k][:], in_=rs[k][:], func=AF.Exp, scale=-0.5),
                  reads=[("rs", k)], writes=[("rs", k)])
            for kc in range(8):
                eng = "dve"
                s.add(eng, lambda e, k=k, kc=kc, tt=tt: e.scalar_tensor_tensor(out=dst_fn(kc, tt), in0=h[:, kc, ts(tt)],
                                                                             scalar=small[:, wcol + kc:wcol + kc + 1], in1=rs[k][:],
                                                                             op0=ALU.mult, op1=ALU.mult),
                      reads=[("h", kc, tt), ("rs", k), "small"], writes=[dst_keys(kc, tt)])

        for tt in range(NT):
            rmsnorm(tt, tt % 2, SB_NMLP, lambda kc, tt: u[:, kc, ts(tt)], lambda kc, tt: ("u", kc, tt), True)

        def load_w(fg):
            k = fg % 2
            s.dma("pool", wu[k][:], wup_d.rearrange("(c p) f -> p c f", p=128)[:, :, fg * 512:(fg + 1) * 512], writes=[("wu", k)])
            s.dma("pool", wd[k][:], wdn_d[fg * 512:(fg + 1) * 512, :].rearrange("(c p) n -> p c n", p=128), writes=[("wd", k)])

        load_w(0)
        ri = 0
        ai = 0
        for fg in range(8):
            if fg + 1 < 8:
                load_w(fg + 1)
            k = fg % 2
            for tt in range(NT):
                a = ai % 2
                ai += 1
                for fc in range(4):
                    b = nps()
                    for kc in range(8):
                        s.pe(lambda e, b=b, k=k, kc=kc, fc=fc, tt=tt: e.matmul(ps[b][:], wu[k][:, kc, fc * 128:(fc + 1) * 128], u[:, kc, ts(tt)],
                                                                          start=(kc == 0), stop=(kc == 7)),
                             reads=[("wu", k), ("u", kc, tt)], writes=[("ps", b)])
                    r = ri % 3
                    ri += 1
                    s.act(lambda e, b=b, r=r: e.activation(out=rl[r][:], in_=ps[b][:], func=AF.Relu),
                          reads=[("ps", b)], writes=[("rl", r)])
                    s.pool(lambda e, r=r, a=a, fc=fc: e.tensor_tensor(out=actb[a][:, fc, :], in0=rl[r][:], in1=rl[r][:], op=ALU.mult),
                           reads=[("rl", r)], writes=[("actb", a, fc)])
                for dc in range(8):
                    b = nps()
                    for fc in range(4):
                        s.pe(lambda e, b=b, k=k, a=a, fc=fc, dc=dc: e.matmul(ps[b][:], wd[k][:, fc, dc * 128:(dc + 1) * 128], actb[a][:, fc, :],
                                                                        start=(fc == 0), stop=(fc == 3)),
                             reads=[("wd", k), ("actb", a, fc)], writes=[("ps", b)])
                    s.dve(lambda e, b=b, dc=dc, tt=tt: e.tensor_tensor(out=h[:, dc, ts(tt)], in0=h[:, dc, ts(tt)], in1=ps[b][:], op=ALU.add),
                          reads=[("ps", b), ("h", dc, tt)], writes=[("h", dc, tt)])

        outk = []
        for tt in range(NT):
            s.dma("sp", hn_d.rearrange("(c p) t -> p c t", p=128)[:, :, ts(tt)], h[:, :, ts(tt)],
                  reads=[("h", c, tt) for c in range(8)], writes=[("hn", tt)])
            outk.append(("hn", tt))
        if last:
            oi = 0
            for tt in range(NT):
                def dst(kc, tt, oi=oi):
                    return ot[(oi + kc) % 2][:]
                k = tt % 2
                s.act(lambda e, k=k, tt=tt: e.activation(out=sq8[k], in_=h[:, :, ts(tt)], func=AF.Square),
                      reads=[("h", c, tt) for c in range(8)], writes=[("sq8", k)])
                b = nps()
                for kc in range(8):
                    s.pe(lambda e, b=b, k=k, kc=kc: e.matmul(ps[b][:], onesm1024, sq8[k][:, kc, :], start=(kc == 0), stop=(kc == 7)),
                         reads=["cst", ("sq8", k)], writes=[("ps", b)])
                s.act(lambda e, b=b, k=k: e.activation(out=rs[k][:], in_=ps[b][:], func=AF.Ln, bias=epsc[:, 0:1]),
                      reads=[("ps", b), "epsc"], writes=[("rs", k)])
                s.act(lambda e, k=k: e.activation(out=rs[k][:], in_=rs[k][:], func=AF.Exp, scale=-0.5),
                      reads=[("rs", k)], writes=[("rs", k)])
                for kc in range(8):
                    o = oi % 2
                    oi += 1
                    s.dve(lambda e, k=k, kc=kc, tt=tt, o=o: e.scalar_tensor_tensor(out=ot[o][:], in0=h[:, kc, ts(tt)],
                                                                              scalar=small[:, SB_NFIN + kc:SB_NFIN + kc + 1], in1=rs[k][:],
                                                                              op0=ALU.mult, op1=ALU.mult),
                          reads=[("h", kc, tt), ("rs", k), "small"], writes=[("ot", o)])
                    s.dma("sp", on_d[kc * 128:(kc + 1) * 128, ts(tt)], ot[o][:], reads=[("ot", o)], writes=[("on", kc, tt)])
                    outk.append(("on", kc, tt))
        s.add("sp", lambda e: None, reads=outk)
        s.finalize_and_emit(st)
    return nc


def consts_A():
    i = np.arange(128)
    same = (i[:,None]//64)==(i[None,:]//64)
    ident = np.eye(128,dtype=np.float32)
    Lincl = (same & (i[:,None]<=i[None,:])).astype(np.float32)
    Ustr = (same & (i[:,None]>i[None,:])).astype(np.float32)
    Bones = same.astype(np.float32)
    NS = np.where(same & (i[:,None]>i[None,:]), 0.0, -30000.0).astype(np.float32)
    NIT = np.where(same & (i[None,:]>=i[:,None]), 0.0, -30000.0).astype(np.float32)
    ones = np.ones((128,128),np.float32)
    cst = np.concatenate([ident,Lincl,Ustr,Bones,NS,NIT,ones],1)
    cstb = np.concatenate([ident, ones/1024, ones, ones*128],1)
    return cst, cstb
def small_A(P, l, seg):
    sm = np.zeros((128,SA_N),np.float32)
    sm[:,SA_NMIX:SA_NMIX+8] = P["norm_mix"][l].reshape(8,128).T
    cw = P["conv_w"][l]
    sm[:,SA_CONV:SA_CONV+48] = cw.reshape(4,12,128).transpose(2,1,0).reshape(128,48)
    sm[:,SA_PSC:SA_PSC+4] = P["pool_scale"][l].reshape(4,128).T
    for g,w in enumerate((2,4,8,16)):
        t = np.arange(16)
        cnt = np.minimum(t+1, w) if seg==0 else np.full(16,w)
        sm[:,SA_INVC+g*16:SA_INVC+(g+1)*16] = (1.0/cnt).astype(np.float32)[None,:]
    sm[:,SA_DTB:SA_DTB+64] = np.tile(P["dt_bias"][l],16)[None,:]
    sm[:,SA_ALOG:SA_ALOG+64] = np.tile(P["a_log"][l],16)[None,:]
    return sm


_CACHE = {}


def _prog(name):
    if name not in _CACHE:
        _CACHE[name] = {"A": lambda: build_A(), "B0": lambda: build_B(False), "B1": lambda: build_B(True)}[name]()
    return _CACHE[name]


def kernel(x, norm_mix, w_in, conv_w, w_pool, pool_scale, a_log, dt_bias, gdn_norm,
           w_out, norm_mlp, w_up, w_down, norm_final):
    P = dict(norm_mix=np.asarray(norm_mix, np.float32), conv_w=np.asarray(conv_w, np.float32),
             pool_scale=np.asarray(pool_scale, np.float32), a_log=np.asarray(a_log, np.float32),
             dt_bias=np.asarray(dt_bias, np.float32))
    x = np.asarray(x, np.float32)
    w_in = np.asarray(w_in, np.float32); w_pool = np.asarray(w_pool, np.float32)
    w_out = np.asarray(w_out, np.float32); w_up = np.asarray(w_up, np.float32); w_down = np.asarray(w_down, np.float32)
    gdn_norm = np.asarray(gdn_norm, np.float32); norm_mlp = np.asarray(norm_mlp, np.float32)
    norm_final = np.asarray(norm_final, np.float32)
    NC = 8
    depth = w_in.shape[0]
    cstA, cstAb = consts_A()
    cstB = np.zeros((128, 256), np.float32); cstB[:, :128] = 1.0 / 1024; cstB[:, 128:] = 1.0 / 128
    hT = [np.ascontiguousarray(x[c // 4, (c % 4) * T:(c % 4 + 1) * T].T) for c in range(NC)]
    out = None
    for l in range(depth):
        ins = []
        for c in range(NC):
            seg = c % 4
            halo = np.ascontiguousarray(hT[c - 1][:, T - HL:]) if seg > 0 else np.zeros((D, HL), np.float32)
            ins.append(dict(hT=hT[c], halo=halo, w_in=w_in[l], w_pool=w_pool[l], smallA=small_A(P, l, seg),
                            cstA=cstA, cstAb=cstAb))
        ra = run_bass_kernel_spmd(_prog("A"), ins, core_ids=list(range(NC))).results
        last = (l == depth - 1)
        ins = []
        for c in range(NC):
            seg = c % 4
            pay = np.zeros((3, 128, 1024), np.float32)
            sm = np.zeros((128, SB_N), np.float32)
            for j in range(3):
                if j < seg:
                    pay[j] = ra[c - seg + j]["spay"]
                    sm[:, SB_CMASK + j] = 1.0
            sm[:, SB_GN] = gdn_norm[l]
            sm[:, SB_NMLP:SB_NMLP + 8] = norm_mlp[l].reshape(8, 128).T
            sm[:, SB_NFIN:SB_NFIN + 8] = norm_final.reshape(8, 128).T
            ins.append(dict(hT=hT[c], mixp=ra[c]["mixp"], ol=ra[c]["ol"], ep=ra[c]["ep"], sz=ra[c]["sz"], pay=pay,
                            smallB=sm, w_out=w_out[l], w_up=w_up[l], w_down=w_down[l], cstB=cstB))
        rb = run_bass_kernel_spmd(_prog("B1" if last else "B0"), ins, core_ids=list(range(NC))).results
        hT = [np.ascontiguousarray(rb[c]["hn"]) for c in range(NC)]
        if last:
            out = np.zeros(x.shape, np.float32)
            for c in range(NC):
                out[c // 4, (c % 4) * T:(c % 4 + 1) * T] = rb[c]["on"].T
    return out
```
